# Optimizing a Trainium2 kernel written in Bass

```python
import jax, jax.numpy as jnp
from jax import lax
import numpy as np

D_MODEL = 2048
BATCH = 4
SEQ = 4096
DEPTH = 2

GRID_W = 64
CTX_LEN = 256
Q_BLOCK = 128
ROPE_THETA = 10000.0
NORM_EPS = 1e-6

A_HEAD_DIM = 128
A_HEADS = D_MODEL // 2 // A_HEAD_DIM
A_KV_HEADS = A_HEADS // 4
B_NOPE_DIM = 128
B_ROPE_DIM = 64
B_V_DIM = 128
B_HEADS = D_MODEL // 4 // B_V_DIM
B_Q_RANK = D_MODEL // 4
B_KV_RANK = D_MODEL // 8
C_WINDOWS = (2, 4, 8, 16)
C_GROUPS = 4
C_GROUP_DIM = D_MODEL // 4 // C_GROUPS
C_WIDTH = C_GROUPS * C_GROUP_DIM

A_Q_COLS = A_HEADS * A_HEAD_DIM
A_KV_COLS = A_KV_HEADS * A_HEAD_DIM
IN_SPLITS = (A_Q_COLS, A_KV_COLS, A_KV_COLS, B_Q_RANK, B_KV_RANK, B_ROPE_DIM, C_WIDTH)
IN_COLS = A_Q_COLS + 2 * A_KV_COLS + B_Q_RANK + B_KV_RANK + B_ROPE_DIM + C_WIDTH
MIX_WIDTH = A_Q_COLS + B_HEADS * B_V_DIM + C_WIDTH
D_FF = ((8 * D_MODEL // 3 + 255) // 256) * 256
N_MOD = 9

kernel_name = 'hybrid_dit_prefix_block'


def rmsnorm(x, g):
    xf = x.astype(jnp.float32)
    y = xf * lax.rsqrt(jnp.mean(xf * xf, axis=-1, keepdims=True) + NORM_EPS)
    return (y * g.astype(jnp.float32)).astype(x.dtype)


def axial_rope_tables(rows, cols, dim):
    n = dim // 4
    inv = ROPE_THETA ** (-jnp.arange(n, dtype=jnp.float32) / n)
    ang = jnp.concatenate([rows[:, None] * inv[None, :], cols[:, None] * inv[None, :]], axis=-1)
    return jnp.cos(ang), jnp.sin(ang)


def apply_rope(x, cos, sin):
    half = x.shape[-1] // 2
    x1, x2 = x[..., :half], x[..., half:]
    cs = cos[None, :, None, :].astype(x.dtype)
    sn = sin[None, :, None, :].astype(x.dtype)
    return jnp.concatenate([x1 * cs - x2 * sn, x1 * sn + x2 * cs], axis=-1)


def swiglu(x, w_in, w_out):
    a, b = jnp.split(x @ w_in, 2, axis=-1)
    return (jax.nn.silu(a) * b) @ w_out


def block_attention(q, k, v):
    B, Sq, Hkv, G, dk = q.shape
    scale = dk ** -0.5
    nb = Sq // Q_BLOCK
    qb = jnp.moveaxis(q.reshape(B, nb, Q_BLOCK, Hkv, G, dk), 1, 0)

    def one_block(qblk):
        s = jnp.einsum('bqhgd,bkhd->bhgqk', qblk, k, preferred_element_type=jnp.float32) * scale
        p = jax.nn.softmax(s, axis=-1).astype(v.dtype)
        return jnp.einsum('bhgqk,bkhd->bqhgd', p, v)

    o = lax.map(one_block, qb)
    return jnp.moveaxis(o, 0, 1).reshape(B, Sq, Hkv * G * v.shape[-1])


def split_cols(u):
    idx, acc = [], 0
    for n in IN_SPLITS[:-1]:
        acc += n
        idx.append(acc)
    return jnp.split(u, idx, axis=-1)


def gqa_q(aq, qg, cos, sin):
    B, L, _ = aq.shape
    q = rmsnorm(aq.reshape(B, L, A_HEADS, A_HEAD_DIM), qg)
    if cos is not None:
        q = apply_rope(q, cos, sin)
    return q.reshape(B, L, A_KV_HEADS, A_HEADS // A_KV_HEADS, A_HEAD_DIM)


def gqa_kv(ak, av, kg, cos, sin):
    B, L, _ = ak.shape
    k = rmsnorm(ak.reshape(B, L, A_KV_HEADS, A_HEAD_DIM), kg)
    if cos is not None:
        k = apply_rope(k, cos, sin)
    return k, av.reshape(B, L, A_KV_HEADS, A_HEAD_DIM)


def mla_q(bq, qg, w_uq, cos, sin):
    B, L, _ = bq.shape
    q = (rmsnorm(bq, qg) @ w_uq).reshape(B, L, B_HEADS, B_NOPE_DIM + B_ROPE_DIM)
    q_nope, q_rope = q[..., :B_NOPE_DIM], q[..., B_NOPE_DIM:]
    if cos is not None:
        q_rope = apply_rope(q_rope, cos, sin)
    return jnp.concatenate([q_nope, q_rope], axis=-1)[:, :, :, None, :]


def mla_kv(bkv, bkr, kvg, w_ukv, cos, sin):
    B, L, _ = bkv.shape
    kv = (rmsnorm(bkv, kvg) @ w_ukv).reshape(B, L, B_HEADS, B_NOPE_DIM + B_V_DIM)
    k_nope, v = kv[..., :B_NOPE_DIM], kv[..., B_NOPE_DIM:]
    k_rope = bkr[:, :, None, :]
    if cos is not None:
        k_rope = apply_rope(k_rope, cos, sin)
    k = jnp.concatenate([k_nope, jnp.broadcast_to(k_rope, (B, L, B_HEADS, B_ROPE_DIM))], axis=-1)
    return k, v


def pool_mix(u, w_pool, scale):
    B, L, _ = u.shape
    ug = u.reshape(B, L, C_GROUPS, C_GROUP_DIM)
    cs = jnp.cumsum(ug.astype(jnp.float32), axis=1)
    cs = jnp.concatenate([jnp.zeros_like(cs[:, :1]), cs], axis=1)
    t = jnp.arange(L)
    means = []
    for g, w in enumerate(C_WINDOWS):
        lo = jnp.clip(t - w // 2, 0, L)
        hi = jnp.clip(t - w // 2 + w, 0, L)
        csg = cs[:, :, g]
        means.append((csg[:, hi] - csg[:, lo]) / (hi - lo).astype(jnp.float32)[None, :, None])
    pooled = jnp.stack(means, axis=2).astype(u.dtype) - ug
    y = jnp.einsum('blgc,gcd->blgd', pooled, w_pool)
    return y.reshape(B, L, C_WIDTH) * scale


def token_mixers(u, uc, rope_a, rope_b, p, ctx_queries):
    cos_a, sin_a = rope_a
    cos_b, sin_b = rope_b
    aq, ak, av, bq, bkv, bkr, cu = split_cols(u)
    aqc, akc, avc, bqc, bkvc, bkrc, cuc = split_cols(uc)
    kA_c, vA_c = gqa_kv(akc, avc, p['a_k_g'], None, None)
    kA, vA = gqa_kv(ak, av, p['a_k_g'], cos_a, sin_a)
    yA = block_attention(gqa_q(aq, p['a_q_g'], cos_a, sin_a),
                         jnp.concatenate([kA_c, kA], axis=1), jnp.concatenate([vA_c, vA], axis=1))
    kB_c, vB_c = mla_kv(bkvc, bkrc, p['b_kv_g'], p['b_w_ukv'], None, None)
    kB, vB = mla_kv(bkv, bkr, p['b_kv_g'], p['b_w_ukv'], cos_b, sin_b)
    yB = block_attention(mla_q(bq, p['b_q_g'], p['b_w_uq'], cos_b, sin_b),
                         jnp.concatenate([kB_c, kB], axis=1), jnp.concatenate([vB_c, vB], axis=1))
    yC = pool_mix(cu, p['c_w_pool'], p['c_scale'])
    y = jnp.concatenate([yA, yB, yC], axis=-1)
    if not ctx_queries:
        return y, None
    yAc = block_attention(gqa_q(aqc, p['a_q_g'], None, None), kA_c, vA_c)
    yBc = block_attention(mla_q(bqc, p['b_q_g'], p['b_w_uq'], None, None), kB_c, vB_c)
    yCc = pool_mix(cuc, p['c_w_pool'], p['c_scale'])
    return y, jnp.concatenate([yAc, yBc, yCc], axis=-1)


def modulated_norm(x, g, shift, scale):
    return rmsnorm(x, g) * (1 + scale) + shift


def layer(h, hc, mod, modc, rope_a, rope_b, p, ctx_out):
    m = jnp.split(mod, N_MOD, axis=-1)
    mc = jnp.split(modc, N_MOD, axis=-1)
    ng = p['norm_g']
    h = h + 0.5 * m[2] * swiglu(modulated_norm(h, ng[0], m[0], m[1]), p['ffn1_in'], p['ffn1_out'])
    hc = hc + 0.5 * mc[2] * swiglu(modulated_norm(hc, ng[0], mc[0], mc[1]), p['ffn1_in'], p['ffn1_out'])
    u = modulated_norm(h, ng[1], m[3], m[4]) @ p['w_in']
    uc = modulated_norm(hc, ng[1], mc[3], mc[4]) @ p['w_in']
    y, yc = token_mixers(u, uc, rope_a, rope_b, p, ctx_out)
    h = h + m[5] * (y @ p['w_out'])
    h = h + 0.5 * m[8] * swiglu(modulated_norm(h, ng[2], m[6], m[7]), p['ffn2_in'], p['ffn2_out'])
    if not ctx_out:
        return h, None
    hc = hc + mc[5] * (yc @ p['w_out'])
    hc = hc + 0.5 * mc[8] * swiglu(modulated_norm(hc, ng[2], mc[6], mc[7]), p['ffn2_in'], p['ffn2_out'])
    return h, hc


def setup_inputs(seed: int = 0) -> dict:
    key = jax.random.key(seed)
    ks = jax.random.split(key, 22)
    f32 = jnp.float32
    D = D_MODEL

    def nrm(k, shape, scale):
        return jax.random.normal(k, shape, f32) * scale

    def gain(k, shape):
        return 1.0 + 0.02 * jax.random.normal(k, shape, f32)

    return {
        'x': nrm(ks[0], (BATCH, SEQ, D), 1.0),
        'c': nrm(ks[1], (BATCH, D), 1.0),
        'ctx': nrm(ks[2], (BATCH, CTX_LEN, D), 1.0),
        'c_ctx': nrm(ks[3], (D,), 1.0),
        'w_mod': nrm(ks[4], (DEPTH, D, N_MOD * D), 0.5 * D ** -0.5),
        'b_mod': nrm(ks[5], (DEPTH, N_MOD * D), 0.01),
        'norm_g': gain(ks[6], (DEPTH, 3, D)),
        'ffn1_in': nrm(ks[7], (DEPTH, D, 2 * D_FF), D ** -0.5),
        'ffn1_out': nrm(ks[8], (DEPTH, D_FF, D), D_FF ** -0.5),
        'w_in': nrm(ks[9], (DEPTH, D, IN_COLS), D ** -0.5),
        'a_q_g': gain(ks[10], (DEPTH, A_HEAD_DIM)),
        'a_k_g': gain(ks[11], (DEPTH, A_HEAD_DIM)),
        'b_q_g': gain(ks[12], (DEPTH, B_Q_RANK)),
        'b_kv_g': gain(ks[13], (DEPTH, B_KV_RANK)),
        'b_w_uq': nrm(ks[14], (DEPTH, B_Q_RANK, B_HEADS * (B_NOPE_DIM + B_ROPE_DIM)), B_Q_RANK ** -0.5),
        'b_w_ukv': nrm(ks[15], (DEPTH, B_KV_RANK, B_HEADS * (B_NOPE_DIM + B_V_DIM)), B_KV_RANK ** -0.5),
        'c_w_pool': nrm(ks[16], (DEPTH, C_GROUPS, C_GROUP_DIM, C_GROUP_DIM), C_GROUP_DIM ** -0.5),
        'c_scale': gain(ks[17], (DEPTH, C_WIDTH)),
        'w_out': nrm(ks[18], (DEPTH, MIX_WIDTH, D), MIX_WIDTH ** -0.5),
        'ffn2_in': nrm(ks[19], (DEPTH, D, 2 * D_FF), D ** -0.5),
        'ffn2_out': nrm(ks[20], (DEPTH, D_FF, D), D_FF ** -0.5),
        'final_g': gain(ks[21], (D,)),
    }


def reference(x, c, ctx, c_ctx, w_mod, b_mod, norm_g, ffn1_in, ffn1_out, w_in, a_q_g, a_k_g,
              b_q_g, b_kv_g, b_w_uq, b_w_ukv, c_w_pool, c_scale, w_out, ffn2_in, ffn2_out, final_g):
    B, S, _ = x.shape
    ROWS = S // GRID_W
    rows = jnp.repeat(jnp.arange(ROWS, dtype=jnp.float32), GRID_W)
    cols = jnp.tile(jnp.arange(GRID_W, dtype=jnp.float32), ROWS)
    rope_a = axial_rope_tables(rows, cols, A_HEAD_DIM)
    rope_b = axial_rope_tables(rows, cols, B_ROPE_DIM)
    c_act = jax.nn.silu(c)
    cc_act = jax.nn.silu(c_ctx)
    h, hc = x, ctx
    for i in range(DEPTH):
        p = {
            'norm_g': norm_g[i], 'ffn1_in': ffn1_in[i], 'ffn1_out': ffn1_out[i], 'w_in': w_in[i],
            'a_q_g': a_q_g[i], 'a_k_g': a_k_g[i], 'b_q_g': b_q_g[i], 'b_kv_g': b_kv_g[i],
            'b_w_uq': b_w_uq[i], 'b_w_ukv': b_w_ukv[i], 'c_w_pool': c_w_pool[i], 'c_scale': c_scale[i],
            'w_out': w_out[i], 'ffn2_in': ffn2_in[i], 'ffn2_out': ffn2_out[i],
        }
        mod = (c_act @ w_mod[i] + b_mod[i])[:, None, :]
        modc = (cc_act @ w_mod[i] + b_mod[i])[None, None, :]
        h, hc = layer(h, hc, mod, modc, rope_a, rope_b, p, i < DEPTH - 1)
    return rmsnorm(h, final_g)
```

```python
import numpy as np
import concourse.bass as bass
import concourse.mybir as mybir
from concourse.bass_utils import run_bass_kernel_spmd

F32, BF16 = mybir.dt.float32, mybir.dt.bfloat16
AF = mybir.ActivationFunctionType
ALU = mybir.AluOpType
AX = mybir.AxisListType

D = 2048
KC = 16
DFF = 5632
NL = 2048
NCX = 256
NT = NL + NCX
EPS = 1e-6
NV = 136
PAIRS = [[0, 1], [2, 3], [4, 5], [6, 7]]
ALL8 = [list(range(8))]
BIGW = [("f1i", 2048, 11264), ("f1o", 5632, 2048), ("win", 2048, 2880), ("wout", 2048, 2048),
        ("f2i", 2048, 11264), ("f2o", 5632, 2048)]


class Buf:
    __slots__ = ("w", "r")

    def __init__(self):
        self.w = {}
        self.r = {}


class Stream:
    def __init__(self, keys, inc):
        self.keys = keys
        self.tot = [0] * len(keys)
        self.i = 0
        self.inc = inc

    def next(self):
        i = self.i % len(self.keys)
        self.i += 1
        prev = self.tot[i]
        self.tot[i] += self.inc
        return self.keys[i], prev, self.tot[i]


class Sched:
    ENG = ("pe", "act", "dve", "pool", "sp")

    def __init__(self):
        self.q = {e: [] for e in self.ENG}
        self.cnt = {e: 0 for e in self.ENG}
        self.seen = {e: {} for e in self.ENG}
        self.semobj = {}

    def _wait(self, eng, deps):
        for k, v in deps.items():
            if self.seen[eng].get(k, 0) < v:
                self.seen[eng][k] = v
                h = self.semobj[k]
                self.q[eng].append(lambda e, h=h, v=v: e.wait_ge(h, v))

    @staticmethod
    def _deps(reads, writes, pwrites):
        deps = {}

        def add(d):
            for k, v in d.items():
                if deps.get(k, 0) < v:
                    deps[k] = v
        for b in reads:
            add(b.w)
        for b in writes:
            add(b.w)
            add(b.r)
        for b in pwrites:
            add(b.r)
        return deps

    @staticmethod
    def _mark(key, tok, reads, writes, pwrites):
        for b in reads:
            if b.r.get(key, 0) < tok:
                b.r[key] = tok
        for b in writes:
            b.w = {key: tok}
            b.r = {}
        for b in pwrites:
            if b.w.get(key, 0) < tok:
                b.w[key] = tok

    def op(self, eng, fn, reads=(), writes=(), pwrites=(), sig=True):
        deps = self._deps(reads, writes, pwrites)
        if eng == "pe":
            deps.pop("pe", None)
        self._wait(eng, deps)
        if sig:
            self.cnt[eng] += 1
            tok = self.cnt[eng]
            h = self.semobj[eng]
            self.q[eng].append(lambda e, fn=fn, h=h: fn(e).then_inc(h, 1))
        else:
            tok = self.cnt[eng] + 1
            self.q[eng].append(lambda e, fn=fn: fn(e))
        self._mark(eng, tok, reads, writes, pwrites)

    def dma(self, eng, stream, fn, reads=(), writes=(), pwrites=()):
        key, prev, tot = stream.next()
        deps = self._deps(reads, writes, pwrites)
        if prev > 0 and deps.get(key, 0) < prev:
            deps[key] = prev
        self._wait(eng, deps)
        h = self.semobj[key]
        inc = stream.inc
        self.q[eng].append(lambda e, fn=fn, h=h, inc=inc: fn(e).then_inc(h, inc))
        self._mark(key, tot, reads, writes, pwrites)

    def wait_all(self, eng, bufs):
        deps = {}
        for b in bufs:
            for k, v in list(b.w.items()) + list(b.r.items()):
                if deps.get(k, 0) < v:
                    deps[k] = v
        if eng == "pe":
            deps.pop("pe", None)
        self._wait(eng, deps)


def build(nlayers=2, stop=None):
    nc = bass.Bass("TRN2", target_bir_lowering=False)
    S = Sched()
    from contextlib import ExitStack
    es = ExitStack()

    def din(name, shape, dt=F32):
        return nc.dram_tensor(name, list(shape), dt, kind="ExternalInput").ap()

    def dscr(name, shape, dt=F32):
        return nc.dram_tensor(name, list(shape), dt).ap()

    def sb(name, shape, dt=F32):
        return es.enter_context(nc.sbuf_tensor(name, list(shape), dt))

    xT = din("xT", [D, NL])
    ctxT = din("ctxT", [D, NCX])
    cT = din("cT", [128, 80])
    selb = din("selb", [128, 5])
    wmod = din("wmod", [2, D, 2304])
    bmod = din("bmod", [128, 36])
    vecs = din("vecs", [128, NV])
    ropeA = din("ropeA", [128, 2, NL])
    ropeB = din("ropeB", [128, 2, NL])
    rots = din("rots", [128, 256])
    invc = din("invc", [128, 4, NT])
    hmask = din("hmask", [128, 2])
    ident = din("ident", [128, 128])
    selv = din("selv", [128, 16])
    wuqn = din("wuqn", [2, 512, 512])
    wuqr = din("wuqr", [2, 512, 256])
    wukk = din("wukk", [2, 256, 512])
    wukv = din("wukv", [2, 256, 512])
    wpool = din("wpool", [2, 512, 128])
    shard, bounce, full = {}, {}, {}
    for l in range(2):
        for nm, R, C in BIGW:
            shard[nm, l] = din(f"{nm}{l}", [R // 8, C])
            bounce[nm, l] = dscr(f"b_{nm}{l}", [R // 8, C])
            full[nm, l] = dscr(f"F_{nm}{l}", [R, C])
    outT = nc.dram_tensor("outT", [D, NL], F32, kind="ExternalOutput").ap()

    hT = dscr("hT", [D, NT])
    QA = dscr("QA", [8, 128, NT], BF16)
    QBn = dscr("QBn", [4, 128, NT], BF16)
    QBr = dscr("QBr", [4, 64, NT], BF16)
    KTs = dscr("KTs", [832, NL], BF16)
    KTg = dscr("KTg", [8 * 832, NL], BF16)
    KTc = dscr("KTc", [832, NCX], BF16)
    Vs = dscr("Vs", [NL, 768], BF16)
    Vg = dscr("Vg", [8 * NL, 768], BF16)
    Vc = dscr("Vc", [NCX, 768], BF16)
    CU = dscr("CU", [512, NT])
    PHs = dscr("PHs", [512, 16])
    PHg = dscr("PHg", [8 * 512, 16])
    yT = dscr("yT", [D, NT], BF16)
    MS = dscr("MS", [4608, 5])
    MG = dscr("MG", [8 * 4608, 5])

    def mksem(key):
        S.semobj[key] = es.enter_context(nc.semaphore(key))
    for e in ("pe", "act", "dve", "pool"):
        mksem(e)

    def mkstream(prefix, n, inc=16):
        keys = [f"{prefix}{i}" for i in range(n)]
        for k in keys:
            mksem(k)
        return Stream(keys, inc)
    st_ld = mkstream("ld", 8)
    st_st = mkstream("st", 8)
    st_w = mkstream("w", 6)
    st_cc = mkstream("cc", 4, inc=1)

    ones32 = sb("ones32", [128, 128])
    ones16 = sb("ones16", [128, 128], BF16)
    epsc = sb("epsc", [128, 1])
    vecs_sb = sb("vecs_sb", [128, NV])
    mv = sb("mv", [128, 2 * 3 * 2 * 3 * 16])
    rots_sb = sb("rots_sb", [128, 256])
    hmask_sb = sb("hmask_sb", [128, 2])
    selv_sb = sb("selv_sb", [128, 16])
    selI = sb("selI", [128, 2, 8, 128], BF16)
    sm = sb("sm", [128, 4, 128])
    PS = [es.enter_context(nc.psum_tensor(f"ps{i}", [128, 512], F32)) for i in range(8)]
    PSB = [Buf() for _ in range(8)]
    B_const = Buf()
    B_sm = [Buf() for _ in range(4)]
    B_mv = Buf()
    B_full = {k: Buf() for k in full}
    B_h = [[Buf() for _ in range(3)] for _ in range(16)]
    B_out = Buf()
    B_QA, B_QBn, B_QBr = Buf(), Buf(), Buf()
    B_KTs, B_KTg, B_KTc, B_Vs, B_Vg, B_Vc = Buf(), Buf(), Buf(), Buf(), Buf(), Buf()
    B_CU, B_PHs, B_PHg, B_yT = Buf(), Buf(), Buf(), Buf()
    streams = [st_ld, st_st, st_w, st_cc]

    def barrier():
        deps = {}
        for stt_ in streams:
            for k, t in zip(stt_.keys, stt_.tot):
                if t > 0:
                    deps[k] = t
        for e in ("pe", "act", "dve", "pool"):
            if S.cnt[e] > 0:
                deps[e] = S.cnt[e]
        for e in S.ENG:
            d2 = dict(deps)
            d2.pop(e, None)
            S._wait(e, d2)

    W, WB = {}, {}
    from contextlib import contextmanager

    @contextmanager
    def workA():
        u = rr("workA", 1000)
        with nc.sbuf_tensor(f"xn{u}", [128, 16, 1152], BF16) as xn_, \
                nc.sbuf_tensor(f"wa{u}", [128, 2, 16, 256], BF16) as wa_, \
                nc.sbuf_tensor(f"wb{u}", [128, 2, 16, 256], BF16) as wb_, \
                nc.sbuf_tensor(f"hst{u}", [128, 2, 16, 128], F32) as hst_, \
                nc.sbuf_tensor(f"sq{u}", [128, 16, 128], F32) as sq_, \
                nc.sbuf_tensor(f"tmpA{u}", [128, 2, 512], F32) as tmpA_:
            W.update(xn=xn_, wa=wa_, wb=wb_, hst=hst_, sq=sq_, tmpA=tmpA_)
            WB.update(xn=[Buf() for _ in range(9)], wa=[Buf(), Buf()], wb=[Buf(), Buf()], hst=[Buf(), Buf()],
                      sq=Buf(), tmpA=[Buf(), Buf()])
            yield
            barrier()

    def mvv(l, n, c, kind):
        o = ((((l * 3 + n) * 2 + c) * 3) + kind) * 16
        return mv[:, o:o + 16]

    ctr = {}

    def rr(name, n):
        v = ctr.get(name, 0)
        ctr[name] = v + 1
        return v % n

    def ld(out, in_, reads=(), writes=(), pwrites=()):
        S.dma("sp", st_ld, lambda e: e.dma_start(out=out, in_=in_), reads, writes, pwrites)

    def stg(out, in_, reads=(), writes=(), pwrites=()):
        S.dma("sp", st_st, lambda e: e.dma_start(out=out, in_=in_), reads, writes, pwrites)

    def wld(out, in_, reads=(), writes=(), pwrites=()):
        S.dma("pool", st_w, lambda e: e.dma_start(out=out, in_=in_), reads, writes, pwrites)

    def coll(groups, in_, out, reads=(), writes=()):
        S.dma("pool", st_cc,
              lambda e: e.collective_compute("AllGather", ALU.bypass, replica_groups=groups,
                                             ins=[in_.opt()], outs=[out.opt()]),
              reads, writes)

    def reserve(ps_i):
        S.wait_all("pe", [PSB[ps_i]])

    def mm(ps_i, ncols, pairs, reads, prows=128):
        reserve(ps_i)
        n = len(pairs)
        for i, (lt, rh) in enumerate(pairs):
            last = i == n - 1
            S.op("pe", lambda e, lt=lt, rh=rh, i=i, last=last: e.matmul(
                PS[ps_i][0:prows, 0:ncols], lhsT=lt, rhs=rh, start=(i == 0), stop=last),
                reads=reads if (i == 0 or last) else (), writes=[PSB[ps_i]] if last else (), sig=last)

    def gather_weights(keys):
        for k in keys:
            stg(bounce[k], shard[k], writes=[B_full[k]])
            coll(ALL8, bounce[k], full[k], reads=[B_full[k]], writes=[B_full[k]])

    def prologue():
        wa = W["wa"]
        B_wa = WB["wa"]
        S.op("dve", lambda e: e.memset(ones32[:], 1.0), writes=[B_const])
        S.op("dve", lambda e: e.memset(ones16[:], 1.0), pwrites=[B_const])
        S.op("dve", lambda e: e.memset(epsc[:], EPS), pwrites=[B_const])
        ld(vecs_sb[:], vecs, pwrites=[B_const])
        ld(rots_sb[:], rots, pwrites=[B_const])
        ld(hmask_sb[:], hmask, pwrites=[B_const])
        ld(selv_sb[:], selv, pwrites=[B_const])
        B_id = Buf()
        wld(wa[:, 0, 0, 0:128], ident, writes=[B_wa[0]])
        for s_i in range(2):
            for r_i in range(8):
                S.op("dve", lambda e, s_i=s_i, r_i=r_i: e.tensor_scalar(
                    out=selI[:, s_i, r_i, :], in0=wa[:, 0, 0, 0:128], scalar1=selv_sb[:, s_i * 8 + r_i:s_i * 8 + r_i + 1],
                    scalar2=None, op0=ALU.mult), reads=[B_wa[0], B_const], pwrites=[B_const])
        gather_weights([("f1i", 0), ("f1o", 0)])
        with nc.sbuf_tensor("cT_sb", [128, 80], F32) as cT_sb, \
                nc.sbuf_tensor("cact", [128, 16, 5], BF16) as cact, \
                nc.sbuf_tensor("selb_sb", [128, 5], F32) as selb_sb, \
                nc.sbuf_tensor("bmod_sb", [128, 36], F32) as bmod_sb, \
                nc.sbuf_tensor("modp", [128, 36, 5], F32) as modp, \
                nc.sbuf_tensor("modall", [128, 2, 144, 5], F32) as modall, \
                nc.sbuf_tensor("modL", [128, 288], F32) as modL:
            B_c, B_modp, B_MS, B_modall, B_modL = Buf(), Buf(), Buf(), Buf(), Buf()
            ld(cT_sb[:], cT, writes=[B_c])
            ld(selb_sb[:], selb, pwrites=[B_c])
            ld(bmod_sb[:], bmod, pwrites=[B_c])
            S.op("act", lambda e: e.activation(out=cact[:].rearrange("p k j -> p (k j)"), in_=cT_sb[:], func=AF.Silu),
                 reads=[B_c], pwrites=[B_c])
            for l in range(2):
                for jj in range(9):
                    s_ = rr("wa", 2)
                    wld(wa[:, s_], wmod[l].rearrange("(kc p) n -> p kc n", p=128)[:, :, jj * 256:(jj + 1) * 256],
                        writes=[B_wa[s_]])
                    for j2 in range(2):
                        j = jj * 2 + j2
                        mm(7, 5, [(wa[:, s_, k, j2 * 128:(j2 + 1) * 128], cact[:, k, :]) for k in range(16)],
                           reads=[B_wa[s_], B_c])
                        S.op("dve", lambda e, l=l, j=j: e.tensor_scalar(
                            out=modp[:, l * 18 + j, :], in0=PS[7][:, 0:5], scalar1=bmod_sb[:, l * 18 + j:l * 18 + j + 1],
                            scalar2=None, op0=ALU.add), reads=[PSB[7], B_c], pwrites=[B_modp])
            with nc.allow_non_contiguous_dma(reason="small mod vectors"):
                stg(MS.rearrange("(c p) f -> p c f", p=128), modp[:], reads=[B_modp], writes=[B_MS])
                coll(ALL8, MS, MG, reads=[B_MS], writes=[B_MS])
                for r in range(8):
                    for l in range(2):
                        ld(modall[:, l, r * 18:(r + 1) * 18, :],
                           MG[r * 4608 + l * 2304: r * 4608 + (l + 1) * 2304, :].rearrange("(j p) f -> p j f", p=128),
                           reads=[B_MS], pwrites=[B_modall])
            mall = modall[:].rearrange("p l c f -> p (l c) f")
            S.op("dve", lambda e: e.tensor_scalar(out=modL[:], in0=mall[:, :, 0], scalar1=selb_sb[:, 0:1], scalar2=None,
                                                  op0=ALU.mult), reads=[B_modall, B_c], writes=[B_modL])
            for j in range(1, 4):
                S.op("dve", lambda e, j=j: e.scalar_tensor_tensor(out=modL[:], in0=mall[:, :, j], scalar=selb_sb[:, j:j + 1],
                                                                  in1=modL[:], op0=ALU.mult, op1=ALU.add),
                     reads=[B_modall, B_c], writes=[B_modL])
            for l in range(2):
                for n in range(3):
                    base = l * 144 + 3 * n * 16
                    gsc = 0.5 if n != 1 else 1.0
                    ng = vecs_sb[:, (l * 3 + n) * 16:(l * 3 + n) * 16 + 16]
                    for c in range(2):
                        if c == 0:
                            sh_, sc_, gt_ = (modL[:, base + i * 16: base + (i + 1) * 16] for i in range(3))
                        else:
                            sh_, sc_, gt_ = (mall[:, base + i * 16: base + (i + 1) * 16, 4] for i in range(3))
                        S.op("dve", lambda e, sc_=sc_, ng=ng, o=mvv(l, n, c, 0): e.scalar_tensor_tensor(
                            out=o, in0=sc_, scalar=1.0, in1=ng, op0=ALU.add, op1=ALU.mult),
                            reads=[B_modL, B_modall, B_const], pwrites=[B_mv])
                        S.op("dve", lambda e, sh_=sh_, o=mvv(l, n, c, 1): e.tensor_copy(out=o, in_=sh_),
                             reads=[B_modL, B_modall], pwrites=[B_mv])
                        S.op("dve", lambda e, gt_=gt_, o=mvv(l, n, c, 2), gsc=gsc: e.tensor_scalar(
                            out=o, in0=gt_, scalar1=gsc, scalar2=None, op0=ALU.mult),
                            reads=[B_modL, B_modall], pwrites=[B_mv])
            barrier()

    def sb_blocks(sbi, with_ctx=True):
        l0 = sbi * 1024
        bl = [(l0, 512, 0, 0), (l0 + 512, 512, 0, 512)]
        if with_ctx:
            bl.append((NL + sbi * 128, 128, 1, 1024))
        return bl

    def hsrc(first, col, n):
        if first:
            if col >= NL:
                return ctxT.rearrange("(kc p) t -> p kc t", p=128)[:, :, col - NL: col - NL + n]
            return xT.rearrange("(kc p) t -> p kc t", p=128)[:, :, col: col + n]
        return hT.rearrange("(kc p) t -> p kc t", p=128)[:, :, col: col + n]

    def hbufs(sbi, is_ctx):
        return [B_h[d][2 if is_ctx else sbi] for d in range(16)]

    def stats_rstd(hs, dim):
        hst, sq = W["hst"], W["sq"]
        B_hst, B_sq = WB["hst"], WB["sq"]
        S.op("act", lambda e: e.activation(out=sq[:], in_=hst[:, hs], func=AF.Square),
             reads=[B_hst[hs]], writes=[B_sq])
        S.op("dve", lambda e: e.tensor_reduce(out=sm[:, 0, :], in_=sq[:].rearrange("p k t -> p t k"),
                                              axis=AX.X, op=ALU.add), reads=[B_sq], writes=[B_sm[0]])
        mm(7, 128, [(ones32[:], sm[:, 0, :])], reads=[B_sm[0], B_const])
        S.op("act", lambda e: e.activation(out=sm[:, 1, :], in_=PS[7][:, 0:128], func=AF.Sqrt,
                                           bias=epsc[:, 0:1], scale=1.0 / dim),
             reads=[PSB[7], B_const], writes=[B_sm[1]])
        S.op("dve", lambda e: e.reciprocal(out=sm[:, 2, :], in_=sm[:, 1, :]), reads=[B_sm[1]], writes=[B_sm[2]])

    def norm_sb(l, n, sbi, first, with_ctx=True):
        hst, xn = W["hst"], W["xn"]
        B_hst, B_xn = WB["hst"], WB["xn"]
        for (c0, nb, is_ctx, x0) in sb_blocks(sbi, with_ctx):
            for t in range(nb // 128):
                col = c0 + t * 128
                xc = x0 + t * 128
                hs = rr("hst", 2)
                ld(hst[:, hs], hsrc(first, col, 128), reads=hbufs(sbi, is_ctx), writes=[B_hst[hs]])
                stats_rstd(hs, D)
                S.op("dve", lambda e, hs=hs: e.tensor_tensor(
                    out=hst[:, hs], in0=hst[:, hs], in1=sm[:, 2, :].unsqueeze(1).to_broadcast([128, 16, 128]),
                    op=ALU.mult), reads=[B_sm[2]], writes=[B_hst[hs]])
                S.op("pool", lambda e, hs=hs, g_=mvv(l, n, is_ctx, 0): e.tensor_tensor(
                    out=hst[:, hs], in0=hst[:, hs], in1=g_.unsqueeze(2).to_broadcast([128, 16, 128]),
                    op=ALU.mult), reads=[B_mv], writes=[B_hst[hs]])
                xb = B_xn[xc // 128]
                S.op("pool", lambda e, hs=hs, s_=mvv(l, n, is_ctx, 1), xc=xc: e.tensor_tensor(
                    out=xn[:, :, xc:xc + 128], in0=hst[:, hs], in1=s_.unsqueeze(2).to_broadcast([128, 16, 128]),
                    op=ALU.add), reads=[B_hst[hs], B_mv], writes=[xb])

    def xbufs(x0, nb):
        return [WB["xn"][i] for i in range(x0 // 128, (x0 + nb + 127) // 128)]

    def resid_update(l, n, d, sbi, blocks, first_src, res, B_res, ws, pairs_fn, reads_fn):
        for bi, (c0, nb, is_ctx, x0) in enumerate(blocks):
            if bi == 1:
                continue
            nn_ = 1024 if bi == 0 else nb
            if first_src:
                src = (ctxT[d * 128:(d + 1) * 128, c0 - NL:c0 - NL + nn_] if is_ctx
                       else xT[d * 128:(d + 1) * 128, c0:c0 + nn_])
            else:
                src = hT[d * 128:(d + 1) * 128, c0:c0 + nn_]
            ld(res[:, ws, x0:x0 + nn_], src, reads=[B_h[d][2 if is_ctx else sbi]],
               writes=[B_res[ws]] if bi == 0 else (), pwrites=() if bi == 0 else [B_res[ws]])
        for bi, (c0, nb, is_ctx, x0) in enumerate(blocks):
            po = 4 + rr("psO", 3)
            mm(po, nb, pairs_fn(x0, nb), reads=reads_fn(bi, x0, nb))
            S.op("dve", lambda e, po=po, nb=nb, x0=x0, ws=ws, gt=mvv(l, n, is_ctx, 2)[:, d:d + 1]:
                 e.scalar_tensor_tensor(out=res[:, ws, x0:x0 + nb], in0=PS[po][:, 0:nb], scalar=gt,
                                        in1=res[:, ws, x0:x0 + nb], op0=ALU.mult, op1=ALU.add),
                 reads=[PSB[po], B_mv], writes=[B_res[ws]])
        for bi, (c0, nb, is_ctx, x0) in enumerate(blocks):
            if bi == 1:
                continue
            nn_ = 1024 if bi == 0 else nb
            stg(hT[d * 128:(d + 1) * 128, c0:c0 + nn_], res[:, ws, x0:x0 + nn_],
                reads=[B_res[ws]], writes=[B_h[d][2 if is_ctx else sbi]])

    def ffn(l, which, first, with_ctx=True):
        xn, wa, wb, tmpA = W["xn"], W["wa"], W["wb"], W["tmpA"]
        B_wa, B_wb, B_tmpA = WB["wa"], WB["wb"], WB["tmpA"]
        n = 0 if which == 1 else 2
        wi = full["f1i" if which == 1 else "f2i", l]
        wo_ = full["f1o" if which == 1 else "f2o", l]
        Bwi = B_full["f1i" if which == 1 else "f2i", l]
        Bwo = B_full["f1o" if which == 1 else "f2o", l]
        wiv = wi.rearrange("(kc p) n -> p kc n", p=128)
        with nc.sbuf_tensor(f"g_{l}_{which}", [128, 22, 1152], BF16) as g, \
                nc.sbuf_tensor(f"wo_{l}_{which}", [128, 2, 22, 128], BF16) as wo, \
                nc.sbuf_tensor(f"res_{l}_{which}", [128, 2, 1152], F32) as res:
            B_g = [[Buf() for _ in range(3)] for _ in range(22)]
            B_wo = [Buf(), Buf()]
            B_res = [Buf(), Buf()]
            for sbi in range(2):
                blocks = sb_blocks(sbi, with_ctx)
                norm_sb(l, n, sbi, first, with_ctx)
                for half in range(2):
                    for fg in range(11):
                        f0 = half * 22 + fg * 2
                        s_ = rr("wa", 2)
                        wld(wa[:, s_], wiv[:, :, f0 * 128: f0 * 128 + 256], reads=[Bwi], writes=[B_wa[s_]])
                        wld(wb[:, s_], wiv[:, :, DFF + f0 * 128: DFF + f0 * 128 + 256], reads=[Bwi], writes=[B_wb[s_]])
                        for j in range(2):
                            fl = fg * 2 + j
                            for bi, (c0, nb, is_ctx, x0) in enumerate(blocks):
                                pa = rr("psA", 2)
                                pb = 2 + rr("psB", 2)
                                mm(pa, nb, [(wa[:, s_, k, j * 128:(j + 1) * 128], xn[:, k, x0:x0 + nb]) for k in range(16)],
                                   reads=[B_wa[s_]] + xbufs(x0, nb))
                                mm(pb, nb, [(wb[:, s_, k, j * 128:(j + 1) * 128], xn[:, k, x0:x0 + nb]) for k in range(16)],
                                   reads=[B_wb[s_]] + xbufs(x0, nb))
                                ts_ = rr("tmpA", 2)
                                S.op("act", lambda e, pa=pa, ts_=ts_, nb=nb: e.activation(
                                    out=tmpA[:, ts_, 0:nb], in_=PS[pa][:, 0:nb], func=AF.Silu),
                                    reads=[PSB[pa]], writes=[B_tmpA[ts_]])
                                S.op("dve", lambda e, pb=pb, ts_=ts_, nb=nb, fl=fl, x0=x0: e.tensor_tensor(
                                    out=g[:, fl, x0:x0 + nb], in0=PS[pb][:, 0:nb], in1=tmpA[:, ts_, 0:nb], op=ALU.mult),
                                    reads=[PSB[pb], B_tmpA[ts_]], writes=[B_g[fl][bi]])
                    for d in range(16):
                        ws = rr("wo", 2)
                        wld(wo[:, ws], wo_.rearrange("(f p) n -> p f n", p=128)[:, half * 22:(half + 1) * 22, d * 128:(d + 1) * 128],
                            reads=[Bwo], writes=[B_wo[ws]])
                        resid_update(l, n, d, sbi, blocks, first and half == 0, res, B_res, ws,
                                     lambda x0, nb, ws=ws: [(wo[:, ws, fl, :], g[:, fl, x0:x0 + nb]) for fl in range(22)],
                                     lambda bi, x0, nb, ws=ws: [B_wo[ws]] + [B_g[fl][bi] for fl in range(22)])
            barrier()

    def mixer_pre(l):
        xn, wa, wb = W["xn"], W["wa"], W["wb"]
        B_wa = WB["wa"]
        wv = full["win", l].rearrange("(kc p) n -> p kc n", p=128)
        Bw = B_full["win", l]
        with nc.sbuf_tensor(f"wuqn_sb{l}", [128, 4, 512], BF16) as wuqn_sb, \
                nc.sbuf_tensor(f"wuqr_sb{l}", [128, 4, 256], BF16) as wuqr_sb, \
                nc.sbuf_tensor(f"wukk_sb{l}", [128, 2, 512], BF16) as wukk_sb, \
                nc.sbuf_tensor(f"wukv_sb{l}", [128, 2, 512], BF16) as wukv_sb, \
                nc.sbuf_tensor(f"ropeA_sb{l}", [128, 2, NL], F32) as ropeA_sb, \
                nc.sbuf_tensor(f"ropeB_sb{l}", [128, 2, NL], F32) as ropeB_sb, \
                nc.sbuf_tensor(f"ub{l}", [128, 4, 512], F32) as ub, \
                nc.sbuf_tensor(f"sqb{l}", [128, 4, 512], F32) as sqb, \
                nc.sbuf_tensor(f"bn{l}", [128, 4, 512], BF16) as bn, \
                nc.sbuf_tensor(f"t32{l}", [128, 3, 512], F32) as t32, \
                nc.sbuf_tensor(f"o16{l}", [128, 2, 512], BF16) as o16, \
                nc.sbuf_tensor(f"vo16{l}", [128, 2, 512], BF16) as vo16:
            B_sw, B_ub, B_sqb, B_bn = Buf(), Buf(), Buf(), Buf()
            B_t32 = [Buf(), Buf(), Buf()]
            B_o16 = [Buf(), Buf()]
            B_vo = [Buf(), Buf()]
            wld(wuqn_sb[:], wuqn[l].rearrange("(c p) n -> p c n", p=128), writes=[B_sw])
            wld(wuqr_sb[:], wuqr[l].rearrange("(c p) n -> p c n", p=128), pwrites=[B_sw])
            wld(wukk_sb[:], wukk[l].rearrange("(c p) n -> p c n", p=128), pwrites=[B_sw])
            wld(wukv_sb[:], wukv[l].rearrange("(c p) n -> p c n", p=128), pwrites=[B_sw])
            ld(ropeA_sb[:], ropeA, pwrites=[B_sw])
            ld(ropeB_sb[:], ropeB, pwrites=[B_sw])

            def kdst(row0, nrows, c0, nb, is_ctx):
                if is_ctx:
                    return KTc[row0:row0 + nrows, c0 - NL:c0 - NL + nb], B_KTc
                return KTs[row0:row0 + nrows, c0:c0 + nb], B_KTs

            def rope_store(pr, nb, is_ctx, c0, rtab, rot_lhsT, dst, dbuf, srows=128):
                os_ = rr("o16", 2)
                if is_ctx:
                    S.op("pool", lambda e: e.tensor_copy(out=o16[0:pr, os_, 0:nb], in_=t32[0:pr, 0, 0:nb]),
                         reads=[B_t32[0]], writes=[B_o16[os_]])
                else:
                    mm(7, nb, [(rot_lhsT, t32[0:pr, 0, 0:nb])], reads=[B_t32[0], B_const], prows=pr)
                    S.op("dve", lambda e: e.tensor_tensor(out=t32[0:pr, 2, 0:nb], in0=PS[7][0:pr, 0:nb],
                                                          in1=rtab[0:pr, 1, c0:c0 + nb], op=ALU.mult),
                         reads=[PSB[7], B_sw], writes=[B_t32[2]])
                    S.op("pool", lambda e: e.tensor_tensor(out=t32[0:pr, 0, 0:nb], in0=t32[0:pr, 0, 0:nb],
                                                           in1=rtab[0:pr, 0, c0:c0 + nb], op=ALU.mult),
                         reads=[B_sw], writes=[B_t32[0]])
                    S.op("pool", lambda e: e.tensor_tensor(out=o16[0:pr, os_, 0:nb], in0=t32[0:pr, 0, 0:nb],
                                                           in1=t32[0:pr, 2, 0:nb], op=ALU.add),
                         reads=[B_t32[0], B_t32[2]], writes=[B_o16[os_]])
                stg(dst, o16[0:srows, os_, 0:nb], reads=[B_o16[os_]], pwrites=[dbuf])

            def head_norm_rope(pu, nb, is_ctx, c0, gain, dst, dbuf):
                S.op("act", lambda e: e.activation(out=t32[:, 0, 0:nb], in_=PS[pu][:, 0:nb], func=AF.Square),
                     reads=[PSB[pu]], writes=[B_t32[0]])
                mm(7, nb, [(ones32[:], t32[:, 0, 0:nb])], reads=[B_t32[0], B_const])
                S.op("act", lambda e: e.activation(out=t32[:, 1, 0:nb], in_=PS[7][:, 0:nb], func=AF.Sqrt,
                                                   bias=epsc[:, 0:1], scale=1.0 / 128),
                     reads=[PSB[7], B_const], writes=[B_t32[1]])
                S.op("dve", lambda e: e.reciprocal(out=t32[:, 1, 0:nb], in_=t32[:, 1, 0:nb]), writes=[B_t32[1]])
                S.op("dve", lambda e: e.scalar_tensor_tensor(out=t32[:, 0, 0:nb], in0=PS[pu][:, 0:nb], scalar=gain,
                                                             in1=t32[:, 1, 0:nb], op0=ALU.mult, op1=ALU.mult),
                     reads=[PSB[pu], B_t32[1], B_const], writes=[B_t32[0]])
                rope_store(128, nb, is_ctx, c0, ropeA_sb, rots_sb[:, 0:128], dst, dbuf)

            def copy_store(pu, pr, nb, dst, dbuf):
                os_ = rr("o16", 2)
                S.op("act", lambda e: e.copy(out=o16[0:pr, os_, 0:nb], in_=PS[pu][0:pr, 0:nb]),
                     reads=[PSB[pu]], writes=[B_o16[os_]])
                stg(dst, o16[0:pr, os_, 0:nb], reads=[B_o16[os_]], pwrites=[dbuf])

            def group_norm(nch, gcol0, s_list, x0, nb, coffs):
                for c in range(nch):
                    pu = rr("psA", 2)
                    s_, coff = s_list[c], coffs[c]
                    mm(pu, nb, [(wa[:, s_, k, coff:coff + 128], xn[:, k, x0:x0 + nb]) for k in range(16)],
                       reads=[B_wa[s_]] + xbufs(x0, nb))
                    S.op("act", lambda e, pu=pu, c=c: e.copy(out=ub[:, c, 0:nb], in_=PS[pu][:, 0:nb]),
                         reads=[PSB[pu]], pwrites=[B_ub] if c else (), writes=() if c else [B_ub])
                    S.op("act", lambda e, pu=pu, c=c: e.activation(out=sqb[:, c, 0:nb], in_=PS[pu][:, 0:nb], func=AF.Square),
                         reads=[PSB[pu]], pwrites=[B_sqb] if c else (), writes=() if c else [B_sqb])
                mm(7, nb, [(ones32[:], sqb[:, c, 0:nb]) for c in range(nch)], reads=[B_sqb, B_const])
                S.op("act", lambda e: e.activation(out=t32[:, 1, 0:nb], in_=PS[7][:, 0:nb], func=AF.Sqrt,
                                                   bias=epsc[:, 0:1], scale=1.0 / (128 * nch)),
                     reads=[PSB[7], B_const], writes=[B_t32[1]])
                S.op("dve", lambda e: e.reciprocal(out=t32[:, 1, 0:nb], in_=t32[:, 1, 0:nb]), writes=[B_t32[1]])
                for c in range(nch):
                    S.op("dve", lambda e, c=c: e.scalar_tensor_tensor(
                        out=bn[:, c, 0:nb], in0=ub[:, c, 0:nb], scalar=vecs_sb[:, gcol0 + c:gcol0 + c + 1],
                        in1=t32[:, 1, 0:nb], op0=ALU.mult, op1=ALU.mult),
                        reads=[B_ub, B_t32[1], B_const], pwrites=[B_bn] if c else (), writes=() if c else [B_bn])

            for sbi in range(2):
                blocks = sb_blocks(sbi)
                norm_sb(l, 1, sbi, False)
                for pi in range(4):
                    s_ = rr("wa", 2)
                    wld(wa[:, s_], wv[:, :, pi * 256:(pi + 1) * 256], reads=[Bw], writes=[B_wa[s_]])
                    for j in range(2):
                        hd = pi * 2 + j
                        for (c0, nb, is_ctx, x0) in blocks:
                            pu = rr("psA", 2)
                            mm(pu, nb, [(wa[:, s_, k, j * 128:(j + 1) * 128], xn[:, k, x0:x0 + nb]) for k in range(16)],
                               reads=[B_wa[s_]] + xbufs(x0, nb))
                            head_norm_rope(pu, nb, is_ctx, c0, vecs_sb[:, 112 + l:113 + l], QA[hd, :, c0:c0 + nb], B_QA)
                s_ = rr("wa", 2)
                wld(wa[:, s_], wv[:, :, 1024:1280], reads=[Bw], writes=[B_wa[s_]])
                for j in range(2):
                    for (c0, nb, is_ctx, x0) in blocks:
                        pu = rr("psA", 2)
                        mm(pu, nb, [(wa[:, s_, k, j * 128:(j + 1) * 128], xn[:, k, x0:x0 + nb]) for k in range(16)],
                           reads=[B_wa[s_]] + xbufs(x0, nb))
                        dst, dbuf = kdst(j * 128, 128, c0, nb, is_ctx)
                        head_norm_rope(pu, nb, is_ctx, c0, vecs_sb[:, 114 + l:115 + l], dst, dbuf)
                s_ = rr("wa", 2)
                wld(wa[:, s_], wv[:, :, 1280:1536], reads=[Bw], writes=[B_wa[s_]])
                for (c0, nb, is_ctx, x0) in blocks:
                    for t in range(nb // 128):
                        pv = 2 + rr("psB", 2)
                        mm(pv, 256, [(xn[:, k, x0 + t * 128:x0 + (t + 1) * 128], wa[:, s_, k, 0:256]) for k in range(16)],
                           reads=[B_wa[s_]] + xbufs(x0, nb))
                        vs = rr("vo", 2)
                        S.op("act", lambda e, pv=pv, vs=vs: e.copy(out=vo16[:, vs, 0:256], in_=PS[pv][:, 0:256]),
                             reads=[PSB[pv]], writes=[B_vo[vs]])
                        if is_ctx:
                            dst, dbuf = Vc[c0 - NL + t * 128:c0 - NL + (t + 1) * 128, 0:256], B_Vc
                        else:
                            dst, dbuf = Vs[c0 + t * 128:c0 + (t + 1) * 128, 0:256], B_Vs
                        stg(dst, vo16[:, vs, 0:256], reads=[B_vo[vs]], pwrites=[dbuf])
                s0 = rr("wa", 2)
                wld(wa[:, s0], wv[:, :, 1536:1792], reads=[Bw], writes=[B_wa[s0]])
                s1 = rr("wa", 2)
                wld(wa[:, s1], wv[:, :, 1792:2048], reads=[Bw], writes=[B_wa[s1]])
                for (c0, nb, is_ctx, x0) in blocks:
                    group_norm(4, 116 + l * 4, [s0, s0, s1, s1], x0, nb, [0, 128, 0, 128])
                    for hd in range(4):
                        pu = rr("psA", 2)
                        mm(pu, nb, [(wuqn_sb[:, c, hd * 128:(hd + 1) * 128], bn[:, c, 0:nb]) for c in range(4)],
                           reads=[B_bn, B_sw])
                        copy_store(pu, 128, nb, QBn[hd, :, c0:c0 + nb], B_QBn)
                    for hp in range(2):
                        pu = rr("psA", 2)
                        mm(pu, nb, [(wuqr_sb[:, c, hp * 128:(hp + 1) * 128], bn[:, c, 0:nb]) for c in range(4)],
                           reads=[B_bn, B_sw])
                        S.op("act", lambda e, pu=pu, nb=nb: e.copy(out=t32[:, 0, 0:nb], in_=PS[pu][:, 0:nb]),
                             reads=[PSB[pu]], writes=[B_t32[0]])
                        rope_store(128, nb, is_ctx, c0, ropeB_sb, rots_sb[:, 128:256],
                                   QBr.rearrange("h r t -> (h r) t")[hp * 128:(hp + 1) * 128, c0:c0 + nb], B_QBr)
                s_ = rr("wa", 2)
                wld(wa[:, s_], wv[:, :, 2048:2304], reads=[Bw], writes=[B_wa[s_]])
                for (c0, nb, is_ctx, x0) in blocks:
                    group_norm(2, 124 + l * 2, [s_, s_], x0, nb, [0, 128])
                    for hd in range(4):
                        pu = rr("psA", 2)
                        mm(pu, nb, [(wukk_sb[:, c, hd * 128:(hd + 1) * 128], bn[:, c, 0:nb]) for c in range(2)],
                           reads=[B_bn, B_sw])
                        dst, dbuf = kdst(256 + hd * 128, 128, c0, nb, is_ctx)
                        copy_store(pu, 128, nb, dst, dbuf)
                    for t in range(nb // 128):
                        pv = 2 + rr("psB", 2)
                        mm(pv, 512, [(bn[:, c, t * 128:(t + 1) * 128], wukv_sb[:, c, :]) for c in range(2)],
                           reads=[B_bn, B_sw])
                        vs = rr("vo", 2)
                        S.op("act", lambda e, pv=pv, vs=vs: e.copy(out=vo16[:, vs, :], in_=PS[pv][:, :]),
                             reads=[PSB[pv]], writes=[B_vo[vs]])
                        if is_ctx:
                            dst, dbuf = Vc[c0 - NL + t * 128:c0 - NL + (t + 1) * 128, 256:768], B_Vc
                        else:
                            dst, dbuf = Vs[c0 + t * 128:c0 + (t + 1) * 128, 256:768], B_Vs
                        stg(dst, vo16[:, vs, :], reads=[B_vo[vs]], pwrites=[dbuf])
                s_ = rr("wa", 2)
                wld(wa[:, s_, :, 0:128], wv[:, :, 2304:2432], reads=[Bw], writes=[B_wa[s_]])
                for (c0, nb, is_ctx, x0) in blocks:
                    pu = rr("psA", 2)
                    mm(pu, nb, [(wa[:, s_, k, 0:128], xn[:, k, x0:x0 + nb]) for k in range(16)],
                       reads=[B_wa[s_]] + xbufs(x0, nb))
                    S.op("act", lambda e, pu=pu, nb=nb: e.copy(out=t32[:, 0, 0:nb], in_=PS[pu][:, 0:nb]),
                         reads=[PSB[pu]], writes=[B_t32[0]])
                    dst, dbuf = kdst(768, 64, c0, nb, is_ctx)
                    rope_store(128, nb, is_ctx, c0, ropeB_sb, rots_sb[:, 128:256], dst, dbuf, srows=64)
                for pi in range(2):
                    s_ = rr("wa", 2)
                    wld(wa[:, s_], wv[:, :, 2368 + pi * 256:2368 + (pi + 1) * 256], reads=[Bw], writes=[B_wa[s_]])
                    for j in range(2):
                        gi = pi * 2 + j
                        for (c0, nb, is_ctx, x0) in blocks:
                            pu = rr("psA", 2)
                            mm(pu, nb, [(wa[:, s_, k, j * 128:(j + 1) * 128], xn[:, k, x0:x0 + nb]) for k in range(16)],
                               reads=[B_wa[s_]] + xbufs(x0, nb))
                            S.op("act", lambda e, pu=pu, nb=nb: e.copy(out=t32[:, 2, 0:nb], in_=PS[pu][:, 0:nb]),
                                 reads=[PSB[pu]], writes=[B_t32[2]])
                            stg(CU[gi * 128:(gi + 1) * 128, c0:c0 + nb], t32[:, 2, 0:nb], reads=[B_t32[2]], pwrites=[B_CU])
            barrier()

    def exchange():
        with nc.allow_non_contiguous_dma(reason="pool halo"):
            stg(PHs[:, 0:8], CU[:, 0:8], reads=[B_CU], writes=[B_PHs])
            stg(PHs[:, 8:16], CU[:, NL - 8:NL], reads=[B_CU], pwrites=[B_PHs])
        coll(ALL8, KTs, KTg, reads=[B_KTs], writes=[B_KTg])
        coll(ALL8, Vs, Vg, reads=[B_Vs], writes=[B_Vg])
        coll(ALL8, PHs, PHg, reads=[B_PHs], writes=[B_PHg])

    def attention(l, with_ctx):
        WINS = (2, 4, 8, 16)
        with nc.sbuf_tensor(f"KA{l}", [128, 2, 4352], BF16) as KA, \
                nc.sbuf_tensor(f"KBn{l}", [128, 4, 4352], BF16) as KBn, \
                nc.sbuf_tensor(f"KBr{l}", [128, 4352], BF16) as KBr, \
                nc.sbuf_tensor(f"V{l}", [128, 34, 768], BF16) as V, \
                nc.sbuf_tensor(f"Q{l}", [128, 2, NT], BF16) as Q, \
                nc.sbuf_tensor(f"Qr{l}", [128, 2, NT], BF16) as Qr, \
                nc.sbuf_tensor(f"pt{l}", [128, 4, 512], BF16) as pt, \
                nc.sbuf_tensor(f"rl{l}", [128, 512], F32) as rl, \
                nc.sbuf_tensor(f"ob{l}", [128, 2, 512], BF16) as ob, \
                nc.sbuf_tensor(f"wp{l}", [128, 4, 128], BF16) as wp:
            B_KV, B_rl, B_xw, B_ic, B_p16, B_wp, B_phs = Buf(), Buf(), Buf(), Buf(), Buf(), Buf(), Buf()
            B_Q = [Buf(), Buf()]
            B_pt = [Buf() for _ in range(4)]
            B_ob = [Buf(), Buf()]
            S.op("dve", lambda e: e.memset(KBr[64:128, :], 0.0), pwrites=[B_KV])
            S.op("dve", lambda e: e.memset(Qr[64:128, :, :], 0.0), pwrites=[B_Q[0], B_Q[1]])
            KTv = KTg.rearrange("(r q) n -> q r n", r=8)
            Vv = Vg.rearrange("(r t p) c -> p r t c", r=8, p=128)
            with nc.sbuf_tensor(f"stg8{l}", [128, 2, 8, 1024], BF16) as stg8:
                B_stg = [Buf(), Buf()]
                S.op("dve", lambda e: e.memset(stg8[:], 0.0), writes=B_stg)

                def sel_rows(row0, nrows, dstf):
                    for half in range(2):
                        ss = rr("stg8", 2)
                        ld(stg8[0:nrows, ss, :, :], KTv[row0:row0 + nrows, :, half * 1024:(half + 1) * 1024],
                           reads=[B_KTg], writes=[B_stg[ss]])
                        for s_i in range(2):
                            for cb in range(2):
                                ps = rr("aS", 3)
                                mm(ps, 512, [(selI[:, s_i, r_i, :], stg8[:, ss, r_i, cb * 512:(cb + 1) * 512]) for r_i in range(8)],
                                   reads=[B_stg[ss], B_const])
                                col = 256 + s_i * NL + half * 1024 + cb * 512
                                S.op("act", lambda e, ps=ps, col=col: e.copy(out=dstf(col), in_=PS[ps][0:nrows, 0:512]),
                                     reads=[PSB[ps]], pwrites=[B_KV])
                for kvh in range(2):
                    ld(KA[:, kvh, 0:256], KTc[kvh * 128:(kvh + 1) * 128, :], reads=[B_KTc], pwrites=[B_KV])
                    sel_rows(kvh * 128, 128, lambda col, kvh=kvh: KA[:, kvh, col:col + 512])
                for hd in range(4):
                    ld(KBn[:, hd, 0:256], KTc[256 + hd * 128:256 + (hd + 1) * 128, :], reads=[B_KTc], pwrites=[B_KV])
                    sel_rows(256 + hd * 128, 128, lambda col, hd=hd: KBn[:, hd, col:col + 512])
                ld(KBr[0:64, 0:256], KTc[768:832, :], reads=[B_KTc], pwrites=[B_KV])
                sel_rows(768, 64, lambda col: KBr[0:64, col:col + 512])
                ld(V[:, 0:2, :], Vc.rearrange("(kt p) c -> p kt c", p=128), reads=[B_Vc], pwrites=[B_KV])
                for t in range(16):
                    ss = rr("stg8", 2)
                    ld(stg8[:, ss, :, 0:768], Vv[:, :, t, :], reads=[B_Vg], writes=[B_stg[ss]])
                    for s_i in range(2):
                        for cp in range(2):
                            ps = rr("aS", 3)
                            mm(ps, 384, [(selI[:, s_i, r_i, :], stg8[:, ss, r_i, cp * 384:(cp + 1) * 384]) for r_i in range(8)],
                               reads=[B_stg[ss], B_const])
                            S.op("act", lambda e, ps=ps, s_i=s_i, t=t, cp=cp: e.copy(
                                out=V[:, 2 + s_i * 16 + t, cp * 384:(cp + 1) * 384], in_=PS[ps][:, 0:384]),
                                reads=[PSB[ps]], pwrites=[B_KV])
                barrier()
            wld(wp[:], wpool[l].rearrange("(g p) n -> p g n", p=128), writes=[B_wp])

            def attn_block(kparts, qbufs, vcol, scale, n, kts, yrow, ycol):
                po = 3 + rr("aO", 2)
                pl = 5 + rr("aL", 2)
                reserve(po)
                reserve(pl)
                nk = len(kts)
                slots = []
                for i in range(nk + 1):
                    if i < nk:
                        ps = rr("aS", 3)
                        mm(ps, n, [(kf(kts[i]), qap) for kf, qap in kparts], reads=[B_KV] + qbufs)
                        sl = rr("pt", 4)
                        S.op("act", lambda e, ps=ps, sl=sl: e.activation(out=pt[:, sl, 0:n], in_=PS[ps][:, 0:n],
                                                                         func=AF.Exp, scale=scale),
                             reads=[PSB[ps]], writes=[B_pt[sl]])
                        slots.append(sl)
                    if i >= 1:
                        sl, kt = slots[i - 1], kts[i - 1]
                        fst, lst = (i - 1 == 0), (i - 1 == nk - 1)
                        S.op("pe", lambda e, sl=sl, kt=kt, fst=fst, lst=lst: e.matmul(
                            PS[po][:, 0:n], lhsT=V[:, kt, vcol:vcol + 128], rhs=pt[:, sl, 0:n], start=fst, stop=lst),
                            reads=[B_pt[sl], B_KV], writes=[PSB[po]] if lst else (), sig=lst)
                        S.op("pe", lambda e, sl=sl, fst=fst, lst=lst: e.matmul(
                            PS[pl][:, 0:n], lhsT=ones16[:], rhs=pt[:, sl, 0:n], start=fst, stop=lst),
                            reads=[B_pt[sl], B_const], writes=[PSB[pl]] if lst else (), sig=True)
                S.op("dve", lambda e: e.reciprocal(out=rl[:, 0:n], in_=PS[pl][:, 0:n]), reads=[PSB[pl]], writes=[B_rl])
                os_ = rr("ob", 2)
                S.op("dve", lambda e: e.tensor_tensor(out=ob[:, os_, 0:n], in0=PS[po][:, 0:n], in1=rl[:, 0:n], op=ALU.mult),
                     reads=[PSB[po], B_rl], writes=[B_ob[os_]])
                stg(yT[yrow:yrow + 128, ycol:ycol + n], ob[:, os_, 0:n], reads=[B_ob[os_]], pwrites=[B_yT])

            qi = 0
            for hd in range(12):
                qs = qi % 2
                qi += 1
                if hd < 8:
                    ld(Q[:, qs, :], QA[hd], reads=[B_QA], writes=[B_Q[qs]])
                    kvh = hd // 4
                    mk = lambda cols: [(lambda kt, kvh=kvh: KA[:, kvh, kt * 128:(kt + 1) * 128], Q[:, qs, cols])]
                    vcol, scale, yrow = kvh * 128, 128 ** -0.5, hd * 128
                else:
                    hb = hd - 8
                    ld(Q[:, qs, :], QBn[hb], reads=[B_QBn], writes=[B_Q[qs]])
                    ld(Qr[0:64, qs, :], QBr[hb], reads=[B_QBr], pwrites=[B_Q[qs]])
                    mk = lambda cols, hb=hb: [(lambda kt: KBn[:, hb, kt * 128:(kt + 1) * 128], Q[:, qs, cols]),
                                              (lambda kt: KBr[:, kt * 128:(kt + 1) * 128], Qr[:, qs, cols])]
                    vcol, scale, yrow = 256 + hb * 128, 192 ** -0.5, 1024 + hb * 128
                for qb in range(4):
                    attn_block(mk(slice(qb * 512, (qb + 1) * 512)), [B_Q[qs]], vcol, scale, 512, list(range(34)),
                               yrow, qb * 512)
                if with_ctx:
                    attn_block(mk(slice(NL, NT)), [B_Q[qs]], vcol, scale, 256, [0, 1], yrow, NL)

            with nc.sbuf_tensor(f"xw{l}", [128, 3, NL + 16], F32) as xw, \
                    nc.sbuf_tensor(f"ic{l}", [128, NT], F32) as ic, \
                    nc.sbuf_tensor(f"p16{l}", [128, NL], BF16) as p16, \
                    nc.sbuf_tensor(f"phs{l}", [128, 8, 16], F32) as phs:
                seqs = [(0, NL, True)] + ([(NL, NCX, False)] if with_ctx else [])
                for gi in range(4):
                    w = WINS[gi]
                    hw = w // 2
                    ld(ic[:], invc[:, gi, :], writes=[B_ic])
                    for (c0, L, is_lat) in seqs:
                        ld(xw[:, 0, 8:8 + L], CU[gi * 128:(gi + 1) * 128, c0:c0 + L], reads=[B_CU], writes=[B_xw])
                        if is_lat:
                            with nc.allow_non_contiguous_dma(reason="pool halo"):
                                ld(phs[:], PHg.rearrange("(r q) c -> q r c", r=8)[gi * 128:(gi + 1) * 128, :, :],
                                   reads=[B_PHg], writes=[B_phs])
                            for (s_i, dc, sc) in ((0, 0, 8), (1, 8 + L, 0)):
                                for r_i in range(8):
                                    if r_i == 0:
                                        S.op("dve", lambda e, s_i=s_i, dc=dc, sc=sc, r_i=r_i: e.tensor_scalar(
                                            out=xw[:, 0, dc:dc + 8], in0=phs[:, r_i, sc:sc + 8],
                                            scalar1=selv_sb[:, s_i * 8 + r_i:s_i * 8 + r_i + 1], scalar2=None, op0=ALU.mult),
                                            reads=[B_phs, B_const], writes=[B_xw])
                                    else:
                                        S.op("dve", lambda e, s_i=s_i, dc=dc, sc=sc, r_i=r_i: e.scalar_tensor_tensor(
                                            out=xw[:, 0, dc:dc + 8], in0=phs[:, r_i, sc:sc + 8],
                                            scalar=selv_sb[:, s_i * 8 + r_i:s_i * 8 + r_i + 1], in1=xw[:, 0, dc:dc + 8],
                                            op0=ALU.mult, op1=ALU.add), reads=[B_phs, B_const], writes=[B_xw])
                            S.op("dve", lambda e: e.tensor_scalar(out=xw[:, 0, 0:8], in0=xw[:, 0, 0:8], scalar1=hmask_sb[:, 0:1],
                                                                  scalar2=None, op0=ALU.mult), reads=[B_const], writes=[B_xw])
                            S.op("dve", lambda e, L=L: e.tensor_scalar(out=xw[:, 0, 8 + L:16 + L], in0=xw[:, 0, 8 + L:16 + L],
                                                                       scalar1=hmask_sb[:, 1:2], scalar2=None, op0=ALU.mult),
                                 reads=[B_const], writes=[B_xw])
                        else:
                            S.op("dve", lambda e: e.memset(xw[:, 0, 0:8], 0.0), writes=[B_xw])
                            S.op("dve", lambda e, L=L: e.memset(xw[:, 0, 8 + L:16 + L], 0.0), writes=[B_xw])
                        cur, width, length = 0, 1, 16 + L
                        while width < w:
                            nxt = 1 if cur != 1 else 2
                            S.op("dve", lambda e, cur=cur, nxt=nxt, width=width, length=length: e.tensor_tensor(
                                out=xw[:, nxt, 0:length - width], in0=xw[:, cur, 0:length - width],
                                in1=xw[:, cur, width:length], op=ALU.add), writes=[B_xw])
                            cur, length, width = nxt, length - width, width * 2
                        oth = 1 if cur != 1 else 2
                        S.op("dve", lambda e, cur=cur, oth=oth, L=L, c0=c0, hw=hw: e.tensor_tensor(
                            out=xw[:, oth, 0:L], in0=xw[:, cur, 8 - hw:8 - hw + L], in1=ic[:, c0:c0 + L], op=ALU.mult),
                            reads=[B_ic], writes=[B_xw])
                        S.op("dve", lambda e, oth=oth, L=L: e.tensor_tensor(
                            out=p16[:, 0:L], in0=xw[:, oth, 0:L], in1=xw[:, 0, 8:8 + L], op=ALU.subtract),
                            reads=[B_xw], writes=[B_p16])
                        for x in range(0, L, 512):
                            nb = min(512, L - x)
                            pu = rr("aS", 3)
                            mm(pu, nb, [(wp[:, gi, :], p16[:, x:x + nb])], reads=[B_wp, B_p16])
                            os_ = rr("ob", 2)
                            S.op("act", lambda e, pu=pu, os_=os_, nb=nb, gi=gi: e.activation(
                                out=ob[:, os_, 0:nb], in_=PS[pu][:, 0:nb], func=AF.Identity,
                                scale=vecs_sb[:, 128 + l * 4 + gi:129 + l * 4 + gi]),
                                reads=[PSB[pu], B_const], writes=[B_ob[os_]])
                            stg(yT[1536 + gi * 128:1536 + (gi + 1) * 128, c0 + x:c0 + x + nb], ob[:, os_, 0:nb],
                                reads=[B_ob[os_]], pwrites=[B_yT])
                barrier()
            barrier()

    def outproj(l, with_ctx):
        xn, wa = W["xn"], W["wa"]
        B_wa, B_xn = WB["wa"], WB["xn"]
        wov = full["wout", l].rearrange("(m p) n -> p m n", p=128)
        with nc.sbuf_tensor(f"reso{l}", [128, 2, 1152], F32) as res:
            B_res = [Buf(), Buf()]
            for sbi in range(2):
                blocks = sb_blocks(sbi, with_ctx)
                for (c0, nb, is_ctx, x0) in blocks:
                    ld(xn[:, :, x0:x0 + nb], yT.rearrange("(m p) t -> p m t", p=128)[:, :, c0:c0 + nb],
                       reads=[B_yT], writes=xbufs(x0, nb))
                for d in range(16):
                    s_ = rr("wa", 2)
                    wld(wa[:, s_, :, 0:128], wov[:, :, d * 128:(d + 1) * 128], reads=[B_full["wout", l]], writes=[B_wa[s_]])
                    resid_update(l, 1, d, sbi, blocks, False, res, B_res, rr("reso", 2),
                                 lambda x0, nb, s_=s_: [(wa[:, s_, m, 0:128], xn[:, m, x0:x0 + nb]) for m in range(16)],
                                 lambda bi, x0, nb, s_=s_: [B_wa[s_]] + xbufs(x0, nb))
            barrier()

    def final_norm():
        hst = W["hst"]
        B_hst = WB["hst"]
        for t in range(NL // 128):
            hs = rr("hst", 2)
            ld(hst[:, hs], hsrc(False, t * 128, 128), reads=hbufs(t // 8, 0), writes=[B_hst[hs]])
            stats_rstd(hs, D)
            S.op("dve", lambda e, hs=hs: e.tensor_tensor(
                out=hst[:, hs], in0=hst[:, hs], in1=sm[:, 2, :].unsqueeze(1).to_broadcast([128, 16, 128]),
                op=ALU.mult), reads=[B_sm[2]], writes=[B_hst[hs]])
            S.op("pool", lambda e, hs=hs: e.tensor_tensor(
                out=hst[:, hs], in0=hst[:, hs], in1=vecs_sb[:, 96:112].unsqueeze(2).to_broadcast([128, 16, 128]),
                op=ALU.mult), reads=[B_const], writes=[B_hst[hs]])
            stg(outT.rearrange("(kc p) t -> p kc t", p=128)[:, :, t * 128:(t + 1) * 128], hst[:, hs],
                reads=[B_hst[hs]], pwrites=[B_out])

    done = False
    with workA():
        prologue()
        gather_weights([("win", 0), ("wout", 0)])
        ffn(0, 1, first=True)
        if stop == ("ffn1", 0):
            done = True
        else:
            gather_weights([("f2i", 0), ("f2o", 0)])
            mixer_pre(0)
            if stop == ("pre0", 0):
                done = True
            else:
                exchange()
            if stop == ("pre", 0):
                done = True
    for l in range(nlayers):
        if done:
            break
        last = l == 1
        if l == 0:
            gather_weights([(nm, 1) for nm, _, _ in BIGW])
        attention(l, with_ctx=not last)
        if stop == ("att", l):
            for d in range(16):
                wld(outT[d * 128:(d + 1) * 128, :], yT[d * 128:(d + 1) * 128, 0:NL], reads=[B_yT], pwrites=[B_out])
            stop = None
            break
        with workA():
            outproj(l, with_ctx=not last)
            if stop == ("mix", l):
                done = True
                break
            ffn(l, 2, first=False, with_ctx=not last)
            if stop == ("ffn2", l):
                done = True
                break
            if not last:
                ffn(l + 1, 1, first=False)
                mixer_pre(l + 1)
                exchange()
            else:
                final_norm()
    if stop is not None:
        for d in range(16):
            stg(outT[d * 128:(d + 1) * 128, :], hT[d * 128:(d + 1) * 128, 0:NL],
                reads=[B_h[d][0], B_h[d][1]], pwrites=[B_out])
    S.wait_all("sp", [B_out])

    with nc.Block() as block:
        @block.tensor
        def _(e):
            for f in S.q["pe"]:
                f(e)

        @block.scalar
        def _(e):
            for f in S.q["act"]:
                f(e)

        @block.vector
        def _(e):
            for f in S.q["dve"]:
                f(e)

        @block.gpsimd
        def _(e):
            for f in S.q["pool"]:
                f(e)

        @block.sync
        def _(e):
            for f in S.q["sp"]:
                f(e)
    es.close()
    return nc


def _rope_tables(dim, tok):
    n = dim // 4
    inv = (np.float32(10000.0) ** (-np.arange(n, dtype=np.float32) / np.float32(n))).astype(np.float32)
    rows = (tok // 64).astype(np.float32)
    cols = (tok % 64).astype(np.float32)
    ang = np.concatenate([rows[:, None] * inv[None, :], cols[:, None] * inv[None, :]], axis=-1)
    return np.cos(ang).astype(np.float32), np.sin(ang).astype(np.float32)


def prep_inputs(x, c, ctx, c_ctx, w_mod, b_mod, norm_g, ffn1_in, ffn1_out, w_in, a_q_g, a_k_g,
                b_q_g, b_kv_g, b_w_uq, b_w_ukv, c_w_pool, c_scale, w_out, ffn2_in, ffn2_out, final_g):
    f32 = np.float32
    A = lambda a: np.ascontiguousarray(np.asarray(a, dtype=f32))

    def pk(v):
        v = np.asarray(v, dtype=f32)
        return v.reshape(-1, 128).T

    vecs = np.zeros((128, NV), f32)
    for l in range(2):
        for n in range(3):
            vecs[:, (l * 3 + n) * 16:(l * 3 + n) * 16 + 16] = pk(norm_g[l, n])
        vecs[:, 112 + l] = a_q_g[l]
        vecs[:, 114 + l] = a_k_g[l]
        vecs[:, 116 + l * 4:116 + l * 4 + 4] = pk(b_q_g[l])
        vecs[:, 124 + l * 2:124 + l * 2 + 2] = pk(b_kv_g[l])
        vecs[:, 128 + l * 4:128 + l * 4 + 4] = pk(c_scale[l])
    vecs[:, 96:112] = pk(final_g)
    call = np.concatenate([np.asarray(c, f32), np.asarray(c_ctx, f32)[None, :]], axis=0)
    cTh = np.ascontiguousarray(call.T.reshape(16, 128, 5).transpose(1, 0, 2).reshape(128, 80))
    rots = np.zeros((128, 256), f32)
    for m in range(64):
        rots[m + 64, m] = -1.0
        rots[m, m + 64] = 1.0
    for blk in range(2):
        for m in range(32):
            rots[blk * 64 + m + 32, 128 + blk * 64 + m] = -1.0
            rots[blk * 64 + m, 128 + blk * 64 + m + 32] = 1.0
    uq = np.asarray(b_w_uq, f32).reshape(2, 512, 4, 192)
    wuqn = A(uq[:, :, :, :128].reshape(2, 512, 512))
    wuqr = A(uq[:, :, :, 128:].reshape(2, 512, 256))
    ukv = np.asarray(b_w_ukv, f32).reshape(2, 256, 4, 256)
    wukk = A(ukv[:, :, :, :128].reshape(2, 256, 512))
    wukv = A(ukv[:, :, :, 128:].reshape(2, 256, 512))
    wpool = A(np.asarray(c_w_pool, f32).reshape(2, 512, 128))
    big = {"f1i": ffn1_in, "f1o": ffn1_out, "win": w_in, "wout": w_out, "f2i": ffn2_in, "f2o": ffn2_out}
    wins = (2, 4, 8, 16)
    in_maps = []
    for r in range(8):
        b, j = r // 2, r % 2
        m = {}
        m["xT"] = A(np.asarray(x[b, j * NL:(j + 1) * NL, :], f32).T)
        m["ctxT"] = A(np.asarray(ctx[b], f32).T)
        m["cT"] = cTh
        sel = np.zeros((128, 5), f32)
        sel[:, b] = 1.0
        m["selb"] = sel
        m["wmod"] = A(np.asarray(w_mod, f32)[:, :, r * 2304:(r + 1) * 2304])
        m["bmod"] = A(np.asarray(b_mod, f32)[:, r * 2304:(r + 1) * 2304].reshape(2, 18, 128).transpose(2, 0, 1).reshape(128, 36))
        m["vecs"] = vecs
        tok = np.arange(j * NL, (j + 1) * NL)
        ca, sa = _rope_tables(128, tok)
        cb, sb_ = _rope_tables(64, tok)
        m["ropeA"] = A(np.stack([np.concatenate([ca, ca], 1).T, np.concatenate([sa, sa], 1).T], axis=1))
        m["ropeB"] = A(np.stack([np.concatenate([cb, cb, cb, cb], 1).T, np.concatenate([sb_, sb_, sb_, sb_], 1).T], axis=1))
        m["rots"] = rots
        ic = np.zeros((4, NT), f32)
        for g, w in enumerate(wins):
            lo = np.clip(tok - w // 2, 0, 4096)
            hi = np.clip(tok - w // 2 + w, 0, 4096)
            ic[g, :NL] = 1.0 / (hi - lo).astype(f32)
            tc_ = np.arange(NCX)
            lo = np.clip(tc_ - w // 2, 0, NCX)
            hi = np.clip(tc_ - w // 2 + w, 0, NCX)
            ic[g, NL:] = 1.0 / (hi - lo).astype(f32)
        m["invc"] = A(np.broadcast_to(ic[None], (128, 4, NT)))
        hm = np.zeros((128, 2), f32)
        hm[:, 0] = 1.0 if j == 1 else 0.0
        hm[:, 1] = 1.0 if j == 0 else 0.0
        m["hmask"] = hm
        m["ident"] = np.eye(128, dtype=f32)
        sv = np.zeros((128, 16), f32)
        sv[:, 0 * 8 + 2 * b] = 1.0
        sv[:, 1 * 8 + 2 * b + 1] = 1.0
        m["selv"] = sv
        m["wuqn"], m["wuqr"], m["wukk"], m["wukv"], m["wpool"] = wuqn, wuqr, wukk, wukv, wpool
        for l in range(2):
            for nm, R, C in BIGW:
                m[f"{nm}{l}"] = A(np.asarray(big[nm][l], f32)[r * (R // 8):(r + 1) * (R // 8), :])
        in_maps.append(m)
    return in_maps


def kernel(**inputs):
    in_maps = prep_inputs(**inputs)
    nc = build()
    res = run_bass_kernel_spmd(nc, in_maps, core_ids=list(range(8)))
    out = np.empty((4, 4096, D), np.float32)
    for r in range(8):
        b, j = r // 2, r % 2
        out[b, j * NL:(j + 1) * NL, :] = res.results[r]["outT"].T
    return out
```
